# Optimizing a Trainium2 kernel written in Bass

```python
import math
import jax, jax.numpy as jnp
from jax import lax
import numpy as np

D_MODEL = 2048
BATCH = 4
SEQ = 2048
DEPTH = 4
DEC_BATCH = 8
DEC_SEQ = 8
PAST_LEN = 16384
PAGE_SIZE = 128

SB_HEAD_DIM = 128
SB_HEADS = D_MODEL // (2 * SB_HEAD_DIM)
SB_WIDTH = SB_HEADS * SB_HEAD_DIM
Q_BLOCK = 128
SB_BIAS_INIT = -7.0
RW_HEAD_DIM = 64
RW_WIDTH = D_MODEL // 2
RW_HEADS = RW_WIDTH // RW_HEAD_DIM
LORA_W = max(32, int(round(1.8 * math.sqrt(RW_WIDTH) / 32)) * 32)
LORA_A = LORA_W
LORA_G = max(32, int(round(0.6 * RW_WIDTH ** 0.8 / 32)) * 32)
RW_IN = 3 * RW_WIDTH + LORA_W + LORA_A + LORA_G
RW_SPLITS = (RW_WIDTH, 2 * RW_WIDTH, 3 * RW_WIDTH, 3 * RW_WIDTH + LORA_W,
             3 * RW_WIDTH + LORA_W + LORA_A)
GN_EPS = 64e-5
EVEN_IN = 3 * SB_WIDTH + RW_IN
CHUNK = 128
C_WIDTH = 2 * D_MODEL
C_GROUPS = 8
C_GROUP_DIM = C_WIDTH // C_GROUPS
FFN_DIM = 2 * D_MODEL
CONV_W = 3
N_EVEN = (DEPTH + 1) // 2
N_ODD = DEPTH // 2
RMS_EPS = 1e-6
LN_EPS = 1e-5

kernel_name = "sb_rwkv7_gmlp_hybrid_decode_step"


def rmsnorm(x, g):
    xf = x.astype(jnp.float32)
    y = xf * lax.rsqrt(jnp.mean(xf * xf, axis=-1, keepdims=True) + RMS_EPS) * g.astype(jnp.float32)
    return y.astype(x.dtype)


def layernorm(x, w, b):
    xf = x.astype(jnp.float32)
    mu = jnp.mean(xf, axis=-1, keepdims=True)
    var = jnp.mean(jnp.square(xf - mu), axis=-1, keepdims=True)
    return ((xf - mu) * lax.rsqrt(var + LN_EPS) * w + b).astype(x.dtype)


def stick_breaking(q, k, v, bias, q_pos, k_pos):
    f32 = jnp.float32
    z = (jnp.einsum('bqhd,bkhd->bhqk', q.astype(f32), k.astype(f32)) * (SB_HEAD_DIM ** -0.5)
         + bias.astype(f32)[None, :, None, None])
    mask = k_pos[None, :] < q_pos[:, None]
    log_stay = jnp.where(mask, jax.nn.log_sigmoid(-z), 0.0)
    log_between = lax.cumsum(log_stay, axis=3, reverse=True) - log_stay
    att = jnp.where(mask, jnp.exp(jax.nn.log_sigmoid(z) + log_between), 0.0)
    return jnp.einsum('bhqk,bkhd->bqhd', att, v.astype(f32))


def sb_prompt(q, k, v, bias):
    Bn, S, H, Dh = q.shape
    nb = S // Q_BLOCK
    qb = jnp.swapaxes(q.reshape(Bn, nb, Q_BLOCK, H, Dh), 0, 1)
    k_pos = jnp.arange(S)

    def one_block(args):
        q_blk, i = args
        q_pos = i * Q_BLOCK + jnp.arange(Q_BLOCK)
        return stick_breaking(q_blk, k, v, bias, q_pos, k_pos)

    ob = lax.map(one_block, (qb, jnp.arange(nb)))
    return jnp.swapaxes(ob, 0, 1).reshape(Bn, S, H, Dh)


def sb_sample(q, k_new, v_new, bias, cache_k, cache_v, layer, page_table):
    Bd, T, H, Dh = q.shape
    past = page_table.shape[1] * PAGE_SIZE
    k_past = cache_k[layer][page_table].reshape(Bd, past, H, Dh)
    v_past = cache_v[layer][page_table].reshape(Bd, past, H, Dh)
    k_all = jnp.concatenate([k_past, k_new.astype(k_past.dtype)], axis=1)
    v_all = jnp.concatenate([v_past, v_new.astype(v_past.dtype)], axis=1)
    q_pos = past + jnp.arange(T)
    k_pos = jnp.arange(past + T)
    return stick_breaking(q, k_all, v_all, bias, q_pos, k_pos)


def rwkv7_mix(zb, z_prev, wkv0, mu, w0, w2, a0, a2, g2, k_k, k_a, r_k, lnx_w, lnx_b):
    f32 = jnp.float32
    Bn, T, _ = zb.shape
    shifted = jnp.concatenate([z_prev[:, None, :].astype(zb.dtype), zb[:, :-1]], axis=1)
    xz = zb + (shifted - zb) * mu
    r, k, v, zw, za, zg = jnp.split(xz, RW_SPLITS, axis=-1)
    w_log = -jax.nn.softplus(-(w0 + jnp.tanh(zw) @ w2).astype(f32)) - 0.5
    decay = jnp.exp(-jnp.exp(w_log))
    a = jax.nn.sigmoid((a0 + za @ a2).astype(f32))
    g = jax.nn.sigmoid(zg) @ g2
    k = k.astype(f32)

    def heads(t):
        return t.astype(f32).reshape(Bn, T, RW_HEADS, RW_HEAD_DIM)

    kk = heads(k * k_k)
    kk = kk / jnp.maximum(jnp.sqrt(jnp.sum(kk * kk, axis=-1, keepdims=True)), 1e-12)
    k_mod = heads(k * (1.0 + (a - 1.0) * k_a))
    r_h, v_h, a_h, w_h = heads(r), heads(v), heads(a), heads(decay)

    def step(S, inp):
        r_t, k_t, v_t, kk_t, b_t, w_t = inp
        sa = jnp.einsum('bhvk,bhk->bhv', S, -kk_t)
        S = (S * w_t[:, :, None, :] + sa[..., None] * b_t[:, :, None, :]
             + v_t[..., None] * k_t[:, :, None, :])
        return S, jnp.einsum('bhvk,bhk->bhv', S, r_t)

    tm = lambda t: jnp.swapaxes(t, 0, 1)
    S_fin, o = lax.scan(step, wkv0.astype(f32),
                        (tm(r_h), tm(k_mod), tm(v_h), tm(kk), tm(kk * a_h), tm(w_h)))
    o = tm(o)
    o_mu = jnp.mean(o, axis=-1, keepdims=True)
    o_var = jnp.mean(jnp.square(o - o_mu), axis=-1, keepdims=True)
    o = ((o - o_mu) * lax.rsqrt(o_var + GN_EPS)).reshape(Bn, T, RW_WIDTH) * lnx_w + lnx_b
    bonus = jnp.sum(r_h * k_mod * r_k, axis=-1, keepdims=True) * v_h
    out = (o + bonus.reshape(Bn, T, RW_WIDTH)) * g
    return out.astype(zb.dtype), zb[:, -1], S_fin.astype(wkv0.dtype)


def chunk_gmlp(h, n_chunks, w_in, ln_w, ln_b, w_s, b_s, w_out):
    Bn, T, _ = h.shape
    L = T // n_chunks
    z = jax.nn.gelu(h @ w_in)
    u, v = jnp.split(z, 2, axis=-1)
    v = layernorm(v, ln_w, ln_b)
    causal = jnp.tril(jnp.ones((L, L), dtype=bool))
    ws = jnp.where(causal, w_s[:, :L, :L], 0.0)
    vg = v.reshape(Bn, n_chunks, L, C_GROUPS, C_GROUP_DIM)
    mixed = jnp.einsum('gts,bcsgd->bctgd', ws, vg) + b_s[:, :L].T[None, None, :, :, None]
    out = u * mixed.reshape(Bn, T, C_WIDTH)
    return (out @ w_out).astype(h.dtype), v


def conv_ffn(h, w_up, conv_w, conv_b, w_down, conv_prev):
    T = h.shape[1]
    up = h @ w_up
    ext = jnp.concatenate([conv_prev.astype(up.dtype), up], axis=1)
    c = conv_b + sum(ext[:, j:j + T] * conv_w[j] for j in range(CONV_W))
    gate, val = jnp.split(c, 2, axis=-1)
    out = (jax.nn.silu(gate) * val) @ w_down
    return out.astype(h.dtype), ext[:, -(CONV_W - 1):]


def split_even(z):
    Bn, T, _ = z.shape
    qkv = z[..., :3 * SB_WIDTH].reshape(Bn, T, 3, SB_HEADS, SB_HEAD_DIM)
    return qkv[:, :, 0], qkv[:, :, 1], qkv[:, :, 2], z[..., 3 * SB_WIDTH:]


def setup_inputs(seed: int = 0) -> dict:
    key = jax.random.key(seed)
    keys = iter(jax.random.split(key, 40))
    f32 = jnp.float32

    def nrm(shape, scale):
        return jax.random.normal(next(keys), shape, f32) * scale

    def near_one(shape):
        return 1.0 + nrm(shape, 0.05)

    n_pages = PAST_LEN // PAGE_SIZE
    n_used = DEC_BATCH * n_pages
    n_pool = n_used + max(1, n_used // 4)
    x_prompt = nrm((BATCH, SEQ, D_MODEL), 1.0)
    x_sample = nrm((DEC_BATCH, DEC_SEQ, D_MODEL), 1.0)
    cache_k = nrm((N_EVEN, n_pool, PAGE_SIZE, SB_HEADS, SB_HEAD_DIM), 1.0)
    cache_v = nrm((N_EVEN, n_pool, PAGE_SIZE, SB_HEADS, SB_HEAD_DIM), 1.0)
    page_table = jax.random.permutation(next(keys), n_pool)[:n_used].reshape(
        DEC_BATCH, n_pages).astype(jnp.int32)
    state_shift = nrm((N_EVEN, DEC_BATCH, RW_IN), 1.0)
    state_wkv = nrm((N_EVEN, DEC_BATCH, RW_HEADS, RW_HEAD_DIM, RW_HEAD_DIM), 0.3)
    state_conv = nrm((DEPTH, DEC_BATCH, CONV_W - 1, 2 * FFN_DIM), 1.0)
    norm_mix = near_one((DEPTH, D_MODEL))
    norm_ffn = near_one((DEPTH, D_MODEL))
    norm_final = near_one((D_MODEL,))
    w_in_even = nrm((N_EVEN, D_MODEL, EVEN_IN), D_MODEL ** -0.5)
    sb_bias = SB_BIAS_INIT + nrm((N_EVEN, SB_HEADS), 0.5)
    mu_shift = jax.random.uniform(next(keys), (N_EVEN, RW_IN), f32)
    w0 = jax.random.uniform(next(keys), (N_EVEN, RW_WIDTH), f32, -6.0, -1.0)
    w2 = nrm((N_EVEN, LORA_W, RW_WIDTH), 0.5 * LORA_W ** -0.5)
    a0 = nrm((N_EVEN, RW_WIDTH), 0.1)
    a2 = nrm((N_EVEN, LORA_A, RW_WIDTH), 0.5 * LORA_A ** -0.5)
    g2 = nrm((N_EVEN, LORA_G, RW_WIDTH), LORA_G ** -0.5)
    k_k = 0.85 + nrm((N_EVEN, RW_WIDTH), 0.05)
    k_a = near_one((N_EVEN, RW_WIDTH))
    r_k = nrm((N_EVEN, RW_HEADS, RW_HEAD_DIM), 0.1)
    lnx_w = near_one((N_EVEN, RW_WIDTH))
    lnx_b = nrm((N_EVEN, RW_WIDTH), 0.01)
    w_out_even = nrm((N_EVEN, SB_WIDTH + RW_WIDTH, D_MODEL), 0.5 * (SB_WIDTH + RW_WIDTH) ** -0.5)
    w_in_odd = nrm((N_ODD, D_MODEL, 2 * C_WIDTH), D_MODEL ** -0.5)
    ln_v_w = near_one((N_ODD, C_WIDTH))
    ln_v_b = nrm((N_ODD, C_WIDTH), 0.01)
    w_spatial = nrm((N_ODD, C_GROUPS, CHUNK, CHUNK), 0.5 * CHUNK ** -0.5)
    b_spatial = 1.0 + nrm((N_ODD, C_GROUPS, CHUNK), 0.1)
    w_out_odd = nrm((N_ODD, C_WIDTH, D_MODEL), 0.5 * C_WIDTH ** -0.5)
    w_up = nrm((DEPTH, D_MODEL, 2 * FFN_DIM), D_MODEL ** -0.5)
    conv_w = nrm((DEPTH, CONV_W, 2 * FFN_DIM), CONV_W ** -0.5)
    conv_b = nrm((DEPTH, 2 * FFN_DIM), 0.01)
    w_down = nrm((DEPTH, FFN_DIM, D_MODEL), 0.5 * FFN_DIM ** -0.5)
    return {"x_prompt": x_prompt, "x_sample": x_sample, "cache_k": cache_k, "cache_v": cache_v,
            "page_table": page_table, "state_shift": state_shift, "state_wkv": state_wkv,
            "state_conv": state_conv, "norm_mix": norm_mix, "norm_ffn": norm_ffn,
            "norm_final": norm_final, "w_in_even": w_in_even, "sb_bias": sb_bias,
            "mu_shift": mu_shift, "w0": w0,
            "w2": w2, "a0": a0, "a2": a2, "g2": g2, "k_k": k_k, "k_a": k_a, "r_k": r_k,
            "lnx_w": lnx_w, "lnx_b": lnx_b, "w_out_even": w_out_even, "w_in_odd": w_in_odd,
            "ln_v_w": ln_v_w, "ln_v_b": ln_v_b, "w_spatial": w_spatial, "b_spatial": b_spatial,
            "w_out_odd": w_out_odd, "w_up": w_up, "conv_w": conv_w, "conv_b": conv_b,
            "w_down": w_down}


def reference(x_prompt, x_sample, cache_k, cache_v, page_table, state_shift, state_wkv,
              state_conv, norm_mix, norm_ffn, norm_final, w_in_even, sb_bias, mu_shift, w0, w2,
              a0, a2, g2, k_k, k_a, r_k, lnx_w, lnx_b, w_out_even, w_in_odd, ln_v_w, ln_v_b,
              w_spatial, b_spatial, w_out_odd, w_up, conv_w, conv_b, w_down):
    xp, xs = x_prompt, x_sample
    Bp, Sp = xp.shape[:2]
    Bs, Ts = xs.shape[:2]
    k_p, v_p, k_s, v_s = [], [], [], []
    sh_p, sh_s, wkv_p, wkv_s = [], [], [], []
    cv_p, cv_s, chunkv_s = [], [], []
    for l in range(DEPTH):
        hp = rmsnorm(xp, norm_mix[l])
        hs = rmsnorm(xs, norm_mix[l])
        i = l // 2
        if l % 2 == 0:
            qp, kp, vp, zbp = split_even(hp @ w_in_even[i])
            qs, ks, vs, zbs = split_even(hs @ w_in_even[i])
            att_p = sb_prompt(qp, kp, vp, sb_bias[i]).astype(xp.dtype).reshape(Bp, Sp, SB_WIDTH)
            att_s = sb_sample(qs, ks, vs, sb_bias[i], cache_k, cache_v, i, page_table).astype(
                xs.dtype).reshape(Bs, Ts, SB_WIDTH)
            rw = (mu_shift[i], w0[i], w2[i], a0[i], a2[i], g2[i], k_k[i], k_a[i], r_k[i],
                  lnx_w[i], lnx_b[i])
            rw_p, shift_p, S_p = rwkv7_mix(zbp, jnp.zeros((Bp, RW_IN), xp.dtype),
                                           jnp.zeros((Bp, RW_HEADS, RW_HEAD_DIM, RW_HEAD_DIM),
                                                     xp.dtype), *rw)
            rw_s, shift_s, S_s = rwkv7_mix(zbs, state_shift[i], state_wkv[i], *rw)
            mp = (jnp.concatenate([att_p, rw_p], axis=-1) @ w_out_even[i]).astype(xp.dtype)
            ms = (jnp.concatenate([att_s, rw_s], axis=-1) @ w_out_even[i]).astype(xs.dtype)
            k_p.append(kp)
            v_p.append(vp)
            k_s.append(ks)
            v_s.append(vs)
            sh_p.append(shift_p)
            sh_s.append(shift_s)
            wkv_p.append(S_p)
            wkv_s.append(S_s)
        else:
            mp, _ = chunk_gmlp(hp, Sp // CHUNK, w_in_odd[i], ln_v_w[i], ln_v_b[i],
                               w_spatial[i], b_spatial[i], w_out_odd[i])
            ms, v_rows = chunk_gmlp(hs, 1, w_in_odd[i], ln_v_w[i], ln_v_b[i],
                                    w_spatial[i], b_spatial[i], w_out_odd[i])
            chunkv_s.append(v_rows)
        xp = xp + mp
        xs = xs + ms
        fp, c_p = conv_ffn(rmsnorm(xp, norm_ffn[l]), w_up[l], conv_w[l], conv_b[l], w_down[l],
                           jnp.zeros((Bp, CONV_W - 1, 2 * FFN_DIM), xp.dtype))
        fs, c_s = conv_ffn(rmsnorm(xs, norm_ffn[l]), w_up[l], conv_w[l], conv_b[l], w_down[l],
                           state_conv[l])
        xp = xp + fp
        xs = xs + fs
        cv_p.append(c_p)
        cv_s.append(c_s)
    y_prompt = rmsnorm(xp, norm_final)
    y_sample = rmsnorm(xs, norm_final)
    return (y_prompt, y_sample, jnp.stack(k_p), jnp.stack(v_p), jnp.stack(k_s), jnp.stack(v_s),
            jnp.stack(sh_p), jnp.stack(sh_s), jnp.stack(wkv_p), jnp.stack(wkv_s),
            jnp.stack(cv_p), jnp.stack(cv_s), jnp.stack(chunkv_s))
```

```python
import contextlib
import math
import numpy as np
import concourse.bass as bass
import concourse.mybir as mybir
from concourse.bass_utils import run_bass_kernel_spmd

F32 = mybir.dt.float32
BF16 = mybir.dt.bfloat16
I32 = mybir.dt.int32
AF = mybir.ActivationFunctionType
ALU = mybir.AluOpType
DSZ = {F32: 4, BF16: 2, I32: 4}
CELL = 64

D = 2048
KD = 16
SBH = 8
RWH = 16
RWW = 1024
RW_IN = 3360
FFN = 4096
CW = 4096
RMS_EPS = 1e-6
LN_EPS = 1e-5
GN_EPS = 64e-5
TS = 8
PAGE = 128


class View:
    __slots__ = ("ap", "space", "lo", "hi")

    def __init__(self, ap, space, lo, hi):
        self.ap, self.space, self.lo, self.hi = ap, space, lo, hi


class Tn:
    def __init__(self, handle, shape, dtype, space, base, part_dim=True):
        self.h, self.shape, self.dtype, self.space, self.base = handle, list(shape), dtype, space, base
        self.esz = DSZ[dtype]
        fs = self.shape[1:] if part_dim else self.shape
        st = [1] * len(fs)
        for i in range(len(fs) - 2, -1, -1):
            st[i] = st[i + 1] * fs[i + 1]
        self.fstr, self.part_dim = st, part_dim

    def __getitem__(self, idx):
        if not isinstance(idx, tuple):
            idx = (idx,)
        idx = list(idx) + [slice(None)] * (len(self.shape) - len(idx))
        fi = idx[1:] if self.part_dim else idx
        fs = self.shape[1:] if self.part_dim else self.shape
        lo = hi = 0
        for s, n, st in zip(fi, fs, self.fstr):
            if isinstance(s, int):
                a, b = s, s + 1
            else:
                a = 0 if s.start is None else s.start
                b = n if s.stop is None else s.stop
            assert 0 <= a < b <= n, (idx, self.shape)
            lo += a * st
            hi += (b - 1) * st
        return View(self.h[tuple(idx)], self.space, self.base + lo * self.esz, self.base + (hi + 1) * self.esz)

    def full(self):
        return self[tuple(slice(None) for _ in self.shape)]

    def raw(self, ap, lo_el=0, hi_el=None):
        n = int(np.prod(self.shape[1:] if self.part_dim else self.shape))
        hi_el = n if hi_el is None else hi_el
        return View(ap, self.space, self.base + lo_el * self.esz, self.base + hi_el * self.esz)


class Op:
    __slots__ = ("eng", "fn", "deps", "needed", "tok", "isdma", "idx")


class Rec:
    ENGS = ["pe", "act", "dve", "pool", "sp"]

    def __init__(self, nc, n_dma_sems=16, strict_same=True):
        self.nc = nc
        self.ops = {e: [] for e in self.ENGS}
        self.lastw = {}
        self.readers = {}
        self.strict_same = strict_same
        self.n_dma_sems = n_dma_sems
        self.nops = 0

    @staticmethod
    def _cells(v):
        if v.space is None:
            return None, ()
        cs = CELL if v.space in ("sb", "ps") else 4096
        return v.space, range(v.lo // cs, (v.hi - 1) // cs + 1)

    def op(self, eng, fn, reads=(), writes=(), dma=False):
        o = Op()
        o.eng, o.fn, o.needed, o.tok, o.isdma = eng, fn, False, None, dma
        deps = set()
        lastw, readers = self.lastw, self.readers
        for v in reads:
            sp, rng = self._cells(v)
            for c in rng:
                k = (sp, c)
                w = lastw.get(k)
                if w is not None:
                    deps.add(w)
                r = readers.get(k)
                if r is None:
                    readers[k] = [o]
                else:
                    r.append(o)
        for v in writes:
            sp, rng = self._cells(v)
            for c in rng:
                k = (sp, c)
                w = lastw.get(k)
                if w is not None:
                    deps.add(w)
                r = readers.get(k)
                if r:
                    deps.update(r)
                    readers[k] = []
                lastw[k] = o
        deps.discard(o)
        best = {}
        out = []
        for d in deps:
            if d.isdma:
                out.append(d)
            else:
                if d.eng == eng and (eng == "pe" or not self.strict_same):
                    continue
                b = best.get(d.eng)
                if b is None or d.idx > b.idx:
                    best[d.eng] = d
        out.extend(best.values())
        for d in out:
            d.needed = True
        o.deps = out
        o.idx = len(self.ops[eng])
        self.ops[eng].append(o)
        self.nops += 1
        return o

    def emit(self):
        nc = self.nc
        with contextlib.ExitStack() as st:
            engsem = {e: st.enter_context(nc.semaphore("es_" + e)) for e in self.ENGS}
            dsem = {q: [st.enter_context(nc.semaphore(f"ds_{q}{i}")) for i in range(self.n_dma_sems)]
                    for q in ("sp", "pool")}
            prewait = {}
            finals = {}
            for e in self.ENGS:
                cnt = 0
                rr = 0
                dcnt = [0] * self.n_dma_sems
                for o in self.ops[e]:
                    if o.isdma:
                        s = rr % self.n_dma_sems
                        rr += 1
                        if dcnt[s] > 0:
                            prewait[o] = (dsem[e][s], dcnt[s])
                        dcnt[s] += 16
                        o.tok = (dsem[e][s], dcnt[s])
                    elif o.needed:
                        cnt += 1
                        o.tok = (engsem[e], cnt)
                if e in ("sp", "pool"):
                    finals[e] = [(dsem[e][i], dcnt[i]) for i in range(self.n_dma_sems) if dcnt[i] > 0]
            block = st.enter_context(nc.Block())
            deco = {"pe": block.tensor, "act": block.scalar, "dve": block.vector, "pool": block.gpsimd,
                    "sp": block.sync}
            for e in self.ENGS:
                def body(eng, ops=self.ops[e], final=finals.get(e, [])):
                    waited = {}
                    for o in ops:
                        ws = []
                        if o in prewait:
                            ws.append(prewait[o])
                        for d in o.deps:
                            ws.append(d.tok)
                        for sem, val in ws:
                            key = id(sem)
                            if waited.get(key, 0) >= val:
                                continue
                            waited[key] = val
                            eng.wait_ge(sem, val)
                        ins = o.fn(eng)
                        if o.isdma:
                            ins.then_inc(o.tok[0], 16)
                        elif o.tok is not None:
                            ins.then_inc(o.tok[0], 1)
                    for sem, val in final:
                        if waited.get(id(sem), 0) < val:
                            eng.wait_ge(sem, val)

                deco[e](body)


def par_layout():
    off = {}
    n = 0

    def add(name, w):
        nonlocal n
        off[name] = n
        n += w

    for l in range(4):
        add(f"nm{l}", 16)
        add(f"nf{l}", 16)
        add(f"cw{l}", 192)
        add(f"cb{l}", 64)
    for i in range(2):
        add(f"mu{i}", 27)
        for nm in ("w0", "a0", "kk", "ka", "rk", "lw", "lb"):
            add(f"{nm}{i}", 8)
        add(f"sb{i}", 8)
        add(f"lvw{i}", 32)
        add(f"lvb{i}", 32)
    add("nfin", 16)
    return off, n


def even_cols():
    chunks = []
    for hg in range(2):
        for part in range(3):
            for hl in range(4):
                h = hg * 4 + hl
                chunks.append(np.arange(part * 1024 + h * 128, part * 1024 + (h + 1) * 128))
    z0 = 3072
    chunks.append(np.arange(z0 + 3072, z0 + 3200))
    chunks.append(np.arange(z0 + 3200, z0 + 3328))
    chunks.append(np.concatenate([np.arange(z0 + 3328, z0 + 3360), -np.ones(96, dtype=np.int64)]))
    chunks.append(-np.ones(128, dtype=np.int64))
    for hp in range(8):
        for part in range(3):
            chunks.append(np.arange(z0 + part * 1024 + hp * 128, z0 + part * 1024 + (hp + 1) * 128))
    return chunks


def take_cols(W, cols):
    out = np.zeros((W.shape[0], len(cols)), dtype=np.float32)
    m = cols >= 0
    out[:, m] = W[:, cols[m]]
    return out


def to_blocks(W):
    K, N = W.shape
    nk = K // 128
    nc_ = 4096 // nk
    assert N % nc_ == 0
    nb = N // nc_
    return np.ascontiguousarray(W.reshape(nk, 128, nb, nc_).transpose(2, 1, 0, 3).reshape(nb, 128, 4096))


def build_wstream(inp):
    blocks = []
    ec = np.concatenate(even_cols())
    for l in range(4):
        i = l // 2
        if l % 2 == 0:
            blocks.append(to_blocks(take_cols(np.asarray(inp["w_in_even"][i]), ec)))
            blocks.append(to_blocks(np.asarray(inp["w_out_even"][i])))
        else:
            wi = np.asarray(inp["w_in_odd"][i])
            blocks.append(to_blocks(wi[:, 4096:8192]))
            blocks.append(to_blocks(wi[:, 0:4096]))
            blocks.append(to_blocks(np.asarray(inp["w_out_odd"][i])))
        wu = np.asarray(inp["w_up"][l])
        order = np.stack([np.arange(32)[:, None] * 128 + np.arange(128)[None, :],
                          4096 + np.arange(32)[:, None] * 128 + np.arange(128)[None, :]], axis=1).reshape(-1)
        blocks.append(to_blocks(wu[:, order]))
        blocks.append(to_blocks(np.asarray(inp["w_down"][l])))
    return np.concatenate(blocks, axis=0)


N_EVEN_IN_CHUNKS = 52
NBLK_EVEN = 26 + 8
NBLK_ODD = 16 + 16 + 16
NBLK_FFN = 32 + 16
NBLK = 2 * (NBLK_EVEN + NBLK_FFN) + 2 * (NBLK_ODD + NBLK_FFN)


def vecT(v, n):
    v = np.asarray(v, dtype=np.float32).reshape(-1)
    out = np.zeros(n * 128, dtype=np.float32)
    out[: v.size] = v
    return out.reshape(n, 128).T


def build_par(inp):
    off, n = par_layout()
    P = np.zeros((128, n), dtype=np.float32)

    def put(name, arr):
        P[:, off[name]: off[name] + arr.shape[1]] = arr

    for l in range(4):
        put(f"nm{l}", vecT(inp["norm_mix"][l], 16))
        put(f"nf{l}", vecT(inp["norm_ffn"][l], 16))
        cw = np.asarray(inp["conv_w"][l])
        put(f"cw{l}", np.concatenate([vecT(cw[j], 64) for j in range(3)], axis=1))
        put(f"cb{l}", vecT(inp["conv_b"][l], 64))
    for i in range(2):
        put(f"mu{i}", vecT(inp["mu_shift"][i], 27))
        for nm, key in (("w0", "w0"), ("a0", "a0"), ("kk", "k_k"), ("ka", "k_a"), ("rk", "r_k"),
                        ("lw", "lnx_w"), ("lb", "lnx_b")):
            put(f"{nm}{i}", vecT(inp[key][i], 8))
        put(f"sb{i}", np.broadcast_to(np.asarray(inp["sb_bias"][i], dtype=np.float32)[None, :], (128, 8)))
        put(f"lvw{i}", vecT(inp["ln_v_w"][i], 32))
        put(f"lvb{i}", vecT(inp["ln_v_b"][i], 32))
    put("nfin", vecT(inp["norm_final"], 16))
    return P


def build_consts(C):
    c = {}
    p = np.arange(128)
    c["ones"] = np.ones((128, 128), np.float32)
    c["tri"] = (p[:, None] < p[None, :]).astype(np.float32)
    c["tri2"] = (p[:, None] >= p[None, :]).astype(np.float32)
    c["blk1"] = ((p[:, None] // 64) == (p[None, :] // 64)).astype(np.float32)
    c["ident"] = np.eye(128, dtype=np.float32)
    t = np.arange(512)
    c["maskA"] = np.stack([((128 * j + p[:, None]) < t[None, :]).astype(np.float32) for j in range(4)], axis=1)
    c["gmask"] = (p[:, None] <= p[None, :]).astype(np.float32)

    def rwmask(Cc):
        CC = 2 * Cc
        q = np.arange(CC)
        same = (q[:, None] // Cc) == (q[None, :] // Cc)
        i, j = q[:, None] % Cc, q[None, :] % Cc
        su = (same & (i < j)).astype(np.float32)
        iu = (same & (i <= j)).astype(np.float32)
        sl = (same & (i > j)).astype(np.float32)
        m = np.zeros((128, 4 * 128), np.float32)
        m[:CC, 0:CC] = su
        m[:CC, CC:2 * CC] = iu
        m[:CC, 256:256 + CC] = sl
        m[:CC, 256 + CC:256 + 2 * CC] = sl
        return m
    c["rwm64"] = rwmask(64)
    c["rwm8"] = rwmask(8)
    r = np.ones((128, 512), np.float32)
    r[:, ::64] = 0.0
    c["reset64"] = r
    r8 = np.ones((128, 8), np.float32)
    r8[:, 0] = 0.0
    c["reset8"] = r8
    s = np.arange(8)
    m8 = (s[:, None] < s[None, :]).astype(np.float32)
    mm = np.zeros((128, 64), np.float32)
    mm[:8] = np.tile(m8, (1, 8))
    c["smask"] = mm
    c["iota"] = np.broadcast_to(p[:, None].astype(np.float32), (128, 1)).copy()
    return c


CONST_F = ["ones", "tri", "tri2", "blk1", "iota"]
CONST_B = ["ident", "maskA", "gmask", "rwm64", "rwm8", "reset64", "reset8", "smask"]


def pack_consts():
    c = build_consts(None)
    res = []
    for order in (CONST_F, CONST_B):
        offs = {}
        cols = []
        n = 0
        for k in order:
            a = c[k].reshape(128, -1)
            offs[k] = (n, a.shape[1])
            n += a.shape[1]
            cols.append(a)
        res.append((np.concatenate(cols, axis=1).astype(np.float32), offs))
    return res


class Builder:
    def __init__(self, SEQ, NPAGES, NPOOL, depth=4, dbg=False):
        self.SEQ, self.NP, self.NPOOL, self.depth = SEQ, NPAGES, NPOOL, depth
        self.NT = SEQ // 512
        self.nc = nc = bass.Bass("TRN2", target_bir_lowering=False)
        self.R = Rec(nc)
        self.poff, self.npar = par_layout()
        (_, self.coff), (_, self.coffb) = pack_consts()
        self.ncst = sum(w for _, w in self.coff.values())
        self.ncstb = sum(w for _, w in self.coffb.values())
        self.n_even = (depth + 1) // 2
        self.n_odd = depth // 2
        self._dram()
        self._sbuf()

    def dram(self, name, shape, dt, kind):
        h = self.nc.dram_tensor(name, list(shape), dt, kind=kind)
        return Tn(h.ap(), shape, dt, None if kind == "ExternalInput" else name, 0, part_dim=False)

    def _dram(self):
        SEQ, NP, NPOOL = self.SEQ, self.NP, self.NPOOL
        I, O = "ExternalInput", "ExternalOutput"
        d = self.dram
        self.xp = d("xp", [128, 16, SEQ], F32, I)
        self.xs = d("xs", [128, 16, TS], F32, I)
        self.wst = d("wst", [NBLK, 128, 4096], F32, I)
        self.par_d = d("par", [128, self.npar], F32, I)
        self.cst_d = d("cst", [128, self.ncst], F32, I)
        self.cstb_d = d("cstb", [128, self.ncstb], F32, I)
        self.lora_d = d("lora", [2, 288, 1024], F32, I)
        self.wsp_d = d("wsp", [2, 8, 128, 128], F32, I)
        self.bsp_d = d("bsp", [2, 128, 1024], F32, I)
        self.cK = d("cacheK", [2, NPOOL * 128, 1024], F32, I)
        self.cV = d("cacheV", [2, NPOOL * 128, 1024], F32, I)
        self.ptab = d("ptab", [128, NP], I32, I)
        self.sshift = d("sshift", [2, 128, 27], F32, I)
        self.swkv = d("swkv", [2, 128, 8, 64], F32, I)
        self.sconv = d("sconv", [4, 128, 2, 64], F32, I)
        self.yp = d("yp", [128, 16, SEQ], F32, O)
        self.ys = d("ys", [128, 16, TS], F32, O)
        self.kp = d("kp", [2, 128, 8, SEQ], F32, O)
        self.vp = d("vp", [2, SEQ, 1024], F32, O)
        self.ks = d("ks", [2, 128, 8, TS], F32, O)
        self.vs = d("vs", [2, TS, 1024], F32, O)
        self.shp = d("shp", [2, 128, 27], F32, O)
        self.shs = d("shs", [2, 128, 27], F32, O)
        self.wkvp = d("wkvp", [2, 128, 8, 128], F32, O)
        self.wkvs = d("wkvs", [2, 128, 8, 128], F32, O)
        self.cvp = d("cvp", [4, 128, 2, 64], F32, O)
        self.cvs = d("cvs", [4, 128, 2, 64], F32, O)
        self.chv = d("chv", [2, 128, 32, TS], F32, O)
        self.kts = d("kts", [2, 128, 8, SEQ], BF16, "Internal")
        self.vsb = d("vsb", [2, SEQ, 1024], BF16, "Internal")

    def sb(self, name, shape, dt, aux=False):
        n = int(np.prod(shape[1:])) * DSZ[dt]
        if aux:
            o = (self.aux_off + 63) // 64 * 64
            self.aux_off = o + n
            assert self.aux_off <= self.aux_end, (name, self.aux_off, self.aux_end)
        else:
            o = (self.sb_off + 63) // 64 * 64
            self.sb_off = o + n
            assert self.sb_off <= self.sb_end, (name, self.sb_off - self.sb_end)
        self._nm = getattr(self, "_nm", 0) + 1
        h = self.nc.alloc_sbuf_tensor_at(f"{name}_{self._nm}", list(shape), dt, offset=o)
        return Tn(h, shape, dt, "sb", o)

    def _sbuf(self):
        nc = self.nc
        total = nc.sbuf_top - nc.sbuf_base - 256
        beg, end = nc.bump_sbuf(total)
        self.sb_off, self.sb_end = beg, end
        sb = self.sb
        self.xT = sb("xT", [128, 16, 512], F32)
        self.hT = sb("hT", [128, 16, 512], BF16)
        self.A = sb("A", [128, 32, 512], BF16)
        self.wb = []
        for s in range(3):
            a = sb(f"wb{s}", [128, 4096], BF16)
            self.wb.append((a, Tn(a.h.rearrange("p (j c) -> p j c", j=16), [128, 16, 256], BF16, "sb", a.base),
                            Tn(a.h.rearrange("p (j c) -> p j c", j=32), [128, 32, 128], BF16, "sb", a.base)))
        self.par = sb("par", [128, self.npar], F32)
        self.par2 = sb("par2", [128, 64], F32)
        self.cst = sb("cst", [128, self.ncst], F32)
        self.cstb = sb("cstb", [128, self.ncstb], BF16)
        io_, _ = self.coffb["ident"]
        self.identb = Tn(self.cstb.h[:, io_:io_ + 128], [128, 128], BF16, "sb", self.cstb.base + io_ * 2)
        st_ = [sb(f"ST32_{i}", [128, 8, 128], F32) for i in range(2)]
        zp_ = [sb(f"zprev{i}", [128, 27], F32) for i in range(2)]
        hl_ = [sb(f"halo{l}", [128, 2, 64], F32) for l in range(4)]
        self.ST32 = [st_, st_]
        self.zprev = [zp_, zp_]
        self.halo = [hl_, hl_]
        self.idx = sb("idx", [128, 2, self.NP], I32)
        self.idxf = sb("idxf", [128, self.NP], F32)
        self.tmpf = [sb(f"tmpf{i}", [128, 514], F32) for i in range(5)]
        self.tmpi = 0
        self.phase0 = self.sb_off
        self.aux0 = self.A.base + 16 * 512 * 2
        self.aux_end = self.A.base + 32 * 512 * 2
        self.aux_off = self.aux0
        self.psf = [Tn(nc.alloc_psum_tensor(f"psf{i}", [128, 512], F32), [128, 512], F32, "ps", i * 2048)
                    for i in range(6)]
        self.psb = [Tn(nc.alloc_psum_tensor(f"psb{i}", [128, 1024], BF16), [128, 1024], BF16, "ps", (6 + i) * 2048)
                    for i in range(2)]
        self.psi = 0
        self.pbi = 0
        self.wcount = 0
        self.wuse = 0

    def phase(self):
        self.sb_off = self.phase0
        self.aux_off = self.aux0

    def C(self, name):
        if name in self.coff:
            o, w = self.coff[name]
            return Tn(self.cst.h[:, o:o + w], [128, w], F32, "sb", self.cst.base + o * 4)
        o, w = self.coffb[name]
        return Tn(self.cstb.h[:, o:o + w], [128, w], BF16, "sb", self.cstb.base + o * 2)

    def P(self, name, j=None, w=1):
        o = self.poff[name] + (0 if j is None else j)
        return self.par[:, o:o + w]

    def psum(self):
        p = self.psf[self.psi % 5]
        self.psi += 1
        return p

    def psum_hold(self):
        return self.psf[5]

    def psumb(self):
        p = self.psb[self.pbi % 2]
        self.pbi += 1
        return p

    def tmp(self):
        t = self.tmpf[self.tmpi % 5]
        self.tmpi += 1
        return t

    def act(self, out, in_, func, bias=None, scale=None):
        kw = {}
        rd = [in_]
        if bias is not None:
            if isinstance(bias, View):
                kw["bias"] = bias.ap
                rd.append(bias)
            else:
                kw["bias"] = float(bias)
        if scale is not None:
            if isinstance(scale, View):
                kw["scale"] = scale.ap
                rd.append(scale)
            else:
                kw["scale"] = float(scale)
        self.R.op("act", lambda e: e.activation(out=out.ap, in_=in_.ap, func=func, **kw), reads=rd, writes=[out])

    def tt(self, eng, out, a, b, op):
        self.R.op(eng, lambda e: e.tensor_tensor(out=out.ap, in0=a.ap, in1=b.ap, op=op), reads=[a, b], writes=[out])

    def ts(self, eng, out, a, s1, s2=None, op0=ALU.mult, op1=None):
        rd = [a]
        v1 = s1.ap if isinstance(s1, View) else float(s1)
        if isinstance(s1, View):
            rd.append(s1)
        if op1 is None:
            self.R.op(eng, lambda e: e.tensor_scalar(out=out.ap, in0=a.ap, scalar1=v1, scalar2=None, op0=op0),
                      reads=rd, writes=[out])
            return
        v2 = s2.ap if isinstance(s2, View) else float(s2)
        if isinstance(s2, View):
            rd.append(s2)
        self.R.op(eng, lambda e: e.tensor_scalar(out=out.ap, in0=a.ap, scalar1=v1, scalar2=v2, op0=op0, op1=op1),
                  reads=rd, writes=[out])

    def stt(self, out, a, s, b, op0, op1):
        rd = [a, b]
        v = s.ap if isinstance(s, View) else float(s)
        if isinstance(s, View):
            rd.append(s)
        self.R.op("dve", lambda e: e.scalar_tensor_tensor(out=out.ap, in0=a.ap, scalar=v, in1=b.ap, op0=op0, op1=op1),
                  reads=rd, writes=[out])

    def cp(self, eng, out, a):
        if eng == "act":
            self.R.op("act", lambda e: e.activation(out=out.ap, in_=a.ap, func=AF.Copy), reads=[a], writes=[out])
        else:
            self.R.op(eng, lambda e: e.tensor_copy(out=out.ap, in_=a.ap), reads=[a], writes=[out])

    def memset(self, eng, out, val):
        self.R.op(eng, lambda e: e.memset(out.ap, val), writes=[out])

    def recip(self, out, a):
        self.R.op("dve", lambda e: e.reciprocal(out=out.ap, in_=a.ap), reads=[a], writes=[out])

    def mm(self, out, lhsT, rhs, start=True, stop=True):
        self.R.op("pe", lambda e: e.matmul(out.ap, lhsT=lhsT.ap, rhs=rhs.ap, start=start, stop=stop),
                  reads=[lhsT, rhs], writes=[out])

    def tr(self, out, in_, ident):
        self.R.op("pe", lambda e: e.transpose(out=out.ap, in_=in_.ap, identity=ident.ap), reads=[in_, ident],
                  writes=[out])

    def dma(self, q, out, in_, extra=()):
        self.R.op(q, lambda e: e.dma_start(out=out.ap, in_=in_.ap), reads=[in_] + list(extra), writes=[out], dma=True)

    def wensure(self, upto):
        while self.wcount < upto:
            g = self.wcount
            blk = g % NBLK
            slot = self.wb[g % 3][0]
            self.dma("pool", slot[:], self.wst[blk])
            self.wcount += 1

    def wblock(self):
        g = self.wuse
        self.wuse += 1
        total = NBLK * (self.NT + 1)
        self.wensure(min(g + 3, total))
        return self.wb[g % 3]

    def skip_blocks(self, n):
        self.wuse += n
        self.wcount = max(self.wcount, self.wuse)

    def lin_chunks(self, nchunks, nk, src, T, fn):
        per = 2 if nk == 16 else 1
        cur = None
        for ci in range(nchunks):
            if ci % per == 0:
                cur = self.wblock()
            wv = cur[1] if nk == 16 else cur[2]
            c0 = (ci % per) * 128
            ps = self.psum()
            for j in range(nk):
                self.mm(ps[:, 0:T], wv[:, j, c0:c0 + 128], src[:, j, 0:T], start=(j == 0), stop=(j == nk - 1))
            fn(ci, ps)

    def rmsnorm(self, T, gname, out, out_f32=False):
        x = self.xT
        ps = self.psum()
        ones = self.C("ones")
        for j in range(16):
            sq = self.tmp()
            self.act(sq[:, 0:T], x[:, j, 0:T], AF.Square)
            self.mm(ps[:, 0:T], ones[:, :], sq[:, 0:T], start=(j == 0), stop=(j == 15))
        rs = self.tmp()
        self.act(rs[:, 0:T], ps[:, 0:T], AF.Ln, bias=RMS_EPS, scale=1.0 / D)
        self.act(rs[:, 0:T], rs[:, 0:T], AF.Exp, scale=-0.5)
        for j in range(16):
            self.stt(out[:, j, 0:T], x[:, j, 0:T], self.P(gname, j), rs[:, 0:T], ALU.mult, ALU.mult)

    def ffn(self, l, T, g):
        self.phase()
        self.rmsnorm(T, f"nf{l}", self.hT)
        halo = self.halo[g][l]
        ext = [[self.sb(f"ext{a}{b}", [128, 514], F32) for b in range(2)] for a in range(2)]
        cbuf = [[self.sb(f"cb{a}{b}", [128, 512], F32) for b in range(2)] for a in range(2)]
        sgb = [self.sb(f"sg{b}", [128, 512], F32) for b in range(2)]
        cwo = self.poff[f"cw{l}"]
        cbo = self.poff[f"cb{l}"]

        def epi(ci, ps):
            fb, half = ci // 2, ci % 2
            blk = fb + 32 * half
            e = ext[half][fb % 2]
            c = cbuf[half][fb % 2]
            self.cp("pool", e[:, 0:2], halo[:, :, blk])
            self.cp("act", e[:, 2:2 + T], ps[:, 0:T])
            self.cp("pool", halo[:, :, blk], e[:, T:T + 2])
            w = [self.par[:, cwo + j * 64 + blk: cwo + j * 64 + blk + 1] for j in range(3)]
            self.ts("dve", c[:, 0:T], e[:, 0:T], w[0], self.par[:, cbo + blk: cbo + blk + 1], ALU.mult, ALU.add)
            self.stt(c[:, 0:T], e[:, 1:T + 1], w[1], c[:, 0:T], ALU.mult, ALU.add)
            self.stt(c[:, 0:T], e[:, 2:T + 2], w[2], c[:, 0:T], ALU.mult, ALU.add)
            if half == 1:
                cg = cbuf[0][fb % 2]
                s = sgb[fb % 2]
                self.act(s[:, 0:T], cg[:, 0:T], AF.Silu)
                self.tt("pool", self.A[:, fb, 0:T], s[:, 0:T], c[:, 0:T], ALU.mult)

        self.lin_chunks(64, 16, self.hT, T, epi)

        def epi2(ob, ps):
            self.tt("dve", self.xT[:, ob, 0:T], self.xT[:, ob, 0:T], ps[:, 0:T], ALU.add)

        self.lin_chunks(16, 32, self.A, T, epi2)

    def gmlp(self, l, T, g):
        i = l // 2
        self.phase()
        L = 128 if g == 0 else TS
        ntb = max(1, T // 128)
        self.rmsnorm(T, f"nm{l}", self.hT)
        vbf = self.sb("vbf", [128, 32, 512], BF16)
        wsT = self.sb("wsT", [128, 8, 128], BF16)
        bsp = self.sb("bsp", [128, 8, 128], F32)
        acc1 = self.sb("acc1", [128, 512], F32)
        acc2 = self.sb("acc2", [128, 512], F32)
        Aa = self.sb("Aa", [128, 512], F32)
        Bb = self.sb("Bb", [128, 512], F32)
        vtk = [self.sb(f"vtk{b}", [128, 512], BF16) for b in range(2)]
        vf = self.sb("vf", [128, 32, TS], F32) if g == 1 else None
        gm = self.C("gmask")
        for gg in range(8):
            t = self.tmp()
            self.dma("sp", t[:, 0:128], self.wsp_d[i, gg])
            self.tt("pool", wsT[:, gg, :], t[:, 0:128], gm[:, :], ALU.mult)
        self.dma("sp", bsp.full(), self.bsp_d[i])
        one = self.C("ones")

        def epi_v(fc, ps):
            gl = self.tmp()
            self.act(gl[:, 0:T], ps[:, 0:T], AF.Gelu_apprx_tanh)
            sq = self.tmp()
            self.tt("pool", sq[:, 0:T], gl[:, 0:T], gl[:, 0:T], ALU.mult)
            if fc == 0:
                self.cp("pool", acc1[:, 0:T], gl[:, 0:T])
                self.cp("pool", acc2[:, 0:T], sq[:, 0:T])
            else:
                self.tt("pool", acc1[:, 0:T], acc1[:, 0:T], gl[:, 0:T], ALU.add)
                self.tt("pool", acc2[:, 0:T], acc2[:, 0:T], sq[:, 0:T], ALU.add)
            if g == 1:
                self.cp("dve", vf[:, fc, 0:T], gl[:, 0:T])
            else:
                self.cp("dve", vbf[:, fc, 0:T], gl[:, 0:T])

        self.lin_chunks(32, 16, self.hT, T, epi_v)
        ps1 = self.psum()
        self.mm(ps1[:, 0:T], one[:, :], acc1[:, 0:T])
        ps2 = self.psum()
        self.mm(ps2[:, 0:T], one[:, :], acc2[:, 0:T])
        mean = self.tmp()
        self.ts("dve", mean[:, 0:T], ps1[:, 0:T], 1.0 / CW)
        m2 = self.tmp()
        self.tt("dve", m2[:, 0:T], mean[:, 0:T], mean[:, 0:T], ALU.mult)
        var = self.tmp()
        self.stt(var[:, 0:T], ps2[:, 0:T], 1.0 / CW, m2[:, 0:T], ALU.mult, ALU.subtract)
        self.act(Aa[:, 0:T], var[:, 0:T], AF.Ln, bias=LN_EPS, scale=1.0)
        self.act(Aa[:, 0:T], Aa[:, 0:T], AF.Exp, scale=-0.5)
        self.stt(Bb[:, 0:T], mean[:, 0:T], -1.0, Aa[:, 0:T], ALU.mult, ALU.mult)
        lvw, lvb = self.poff[f"lvw{i}"], self.poff[f"lvb{i}"]
        for fc in range(32):
            t1 = self.tmp()
            src = vf[:, fc, 0:T] if g == 1 else vbf[:, fc, 0:T]
            self.tt("dve", t1[:, 0:T], src, Aa[:, 0:T], ALU.mult)
            self.tt("pool", t1[:, 0:T], t1[:, 0:T], Bb[:, 0:T], ALU.add)
            if g == 1:
                self.act(vf[:, fc, 0:T], t1[:, 0:T], AF.Identity, bias=self.par[:, lvb + fc:lvb + fc + 1],
                         scale=self.par[:, lvw + fc:lvw + fc + 1])
                self.cp("pool", vbf[:, fc, 0:T], vf[:, fc, 0:T])
            else:
                self.act(vbf[:, fc, 0:T], t1[:, 0:T], AF.Identity, bias=self.par[:, lvb + fc:lvb + fc + 1],
                         scale=self.par[:, lvw + fc:lvw + fc + 1])
        if g == 1:
            self.dma("sp", self.chv[i], vf[:])
        A = self.A
        for tb in range(ntb):
            t0 = tb * 128
            for gg in range(8):
                pb = self.psumb()
                for k in range(4):
                    self.tr(pb[0:L, k * 128:(k + 1) * 128], vbf[:, 4 * gg + k, t0:t0 + L], self.identb[:, :])
                vt = vtk[(tb * 8 + gg) % 2]
                self.cp("act", vt[0:L, :], pb[0:L, 0:512])
                ps = self.psum()
                for k in range(4):
                    self.mm(ps[:, k * 128:k * 128 + L], vt[0:L, k * 128:(k + 1) * 128], wsT[0:L, gg, 0:L])
                o_ap = A.h[:, 4 * gg:4 * gg + 4, t0:t0 + L]
                i0 = ps.h.rearrange("p (k t) -> p k t", k=4)[:, :, 0:L]
                i1 = bsp.h[:, gg:gg + 1, 0:L].broadcast_to([128, 4, L])
                self.R.op("dve", lambda e, o_ap=o_ap, i0=i0, i1=i1: e.tensor_tensor(out=o_ap, in0=i0, in1=i1, op=ALU.add),
                          reads=[ps[:, :], bsp[:, gg, :]], writes=[A[:, 4 * gg:4 * gg + 4, t0:t0 + L]])

        def epi_u(fc, ps):
            gl = self.tmp()
            self.act(gl[:, 0:T], ps[:, 0:T], AF.Gelu_apprx_tanh)
            self.tt("pool", A[:, fc, 0:T], gl[:, 0:T], A[:, fc, 0:T], ALU.mult)

        self.lin_chunks(32, 16, self.hT, T, epi_u)

        def epi_o(ob, ps):
            self.tt("dve", self.xT[:, ob, 0:T], self.xT[:, ob, 0:T], ps[:, 0:T], ALU.add)

        self.lin_chunks(16, 32, A, T, epi_o)

    def sb_prompt_group(self, i, hg, T, tok0, KT, Vb, qT, bufs):
        nkb = tok0 // 128 + 4
        e_, sp_, w_, att_, S_, en = bufs
        tri2, ones, mA = self.C("tri2"), self.C("ones"), self.C("maskA")
        mk = Tn(mA.h.rearrange("p (j t) -> p j t", j=4), [128, 4, 512], BF16, "sb", mA.base)
        sbo = self.poff[f"sb{i}"]
        items = [(hl, kb) for hl in range(4) for kb in reversed(range(nkb))]

        def stageA(n):
            hl, kb = items[n]
            h = hg * 4 + hl
            j = kb - tok0 // 128
            e, sp = e_[n % 2], sp_[n % 2]
            ps1 = self.psf[n % 2]
            self.mm(ps1[:, :], KT[:, hl, kb * 128:(kb + 1) * 128], qT[:, hl, :])
            self.act(e[:, :], ps1[:, :], AF.Exp, bias=self.par[:, sbo + h:sbo + h + 1], scale=128 ** -0.5)
            self.act(sp[:, :], e[:, :], AF.Ln, bias=1.0)
            if j >= 0:
                self.tt("dve", e[:, :], e[:, :], mk[:, j, :], ALU.mult)
                self.tt("pool", sp[:, :], sp[:, :], mk[:, j, :], ALU.mult)

        def stageB(n):
            hl, kb = items[n]
            first, lastk = (kb == nkb - 1), (kb == 0)
            e, sp, w, att = e_[n % 2], sp_[n % 2], w_[n % 2], att_[n % 3]
            ps2 = self.psf[2 + n % 2]
            Sold, Snew = S_[kb % 2], S_[(kb + 1) % 2]
            self.mm(ps2[:, :], tri2[:, :], sp[:, :], start=True, stop=first)
            if not first:
                self.mm(ps2[:, :], ones[:, :], Sold[:, :], start=False, stop=True)
            self.act(w[:, :], ps2[:, :], AF.Exp, scale=-1.0)
            self.tt("dve", att[:, :], e[:, :], w[:, :], ALU.mult)
            if not lastk:
                if first:
                    self.cp("pool", Snew[:, :], sp[:, :])
                else:
                    self.tt("pool", Snew[:, :], Sold[:, :], sp[:, :], ALU.add)

        def stageC(n):
            hl, kb = items[n]
            h = hg * 4 + hl
            first, lastk = (kb == nkb - 1), (kb == 0)
            att = att_[n % 3]
            pso = self.psf[4 + hl % 2]
            self.mm(pso[:, :], Vb[:, kb, hl * 128:(hl + 1) * 128], att[:, :], start=first, stop=lastk)
            if lastk:
                self.cp("act", self.A[:, h, 0:T], pso[:, :])

        N = len(items)
        stageA(0)
        for n in range(N):
            if n + 1 < N:
                stageA(n + 1)
            stageB(n)
            if n >= 1:
                stageC(n - 1)
        stageC(N - 1)

    def lin_T(self, cur, src, T, tbs, fn):
        wv = cur[1]
        for tb in range(tbs):
            nt = min(128, T)
            ps = self.psum()
            for j in range(16):
                self.mm(ps[0:nt, 0:256], src[:, j, tb * 128:tb * 128 + nt], wv[:, j, :], start=(j == 0), stop=(j == 15))
            fn(tb, ps, nt)

    def even_attn_proj(self, i, T, g, tok0, hg, qT, KT, Vb, kst, vst):
        tbs = max(1, T // 128)
        kcol0 = tok0 if g == 0 else 0
        for part in range(3):
            for half in range(2):
                cur = self.wblock()
                if part < 2:
                    wv = cur[1]
                    for c in range(2):
                        hl = half * 2 + c
                        h = hg * 4 + hl
                        ps = self.psum()
                        for j in range(16):
                            self.mm(ps[:, 0:T], wv[:, j, c * 128:(c + 1) * 128], self.hT[:, j, 0:T], start=(j == 0),
                                    stop=(j == 15))
                        if part == 0:
                            hq = hl if g == 0 else h
                            self.cp("act", qT[:, hq, 0:T], ps[:, 0:T])
                        else:
                            kf = kst[(hl) % 2]
                            self.cp("act", kf[:, 0:T], ps[:, 0:T])
                            if g == 0:
                                self.dma("sp", self.kp[i, :, h, tok0:tok0 + T], kf[:, 0:T])
                                self.cp("dve", KT[:, hl, tok0:tok0 + T], kf[:, 0:T])
                            else:
                                self.dma("sp", self.ks[i, :, h, :], kf[:, 0:T])
                                self.cp("dve", KT[:, h, 0:T], kf[:, 0:T])
                else:
                    def fv(tb, ps, nt, half=half):
                        vs_ = vst[(tb + half) % 2]
                        self.cp("act", vs_[0:nt, 0:256], ps[0:nt, 0:256])
                        c0 = hg * 512 + half * 256
                        if g == 0:
                            self.dma("sp", self.vp[i, tok0 + tb * 128: tok0 + tb * 128 + nt, c0:c0 + 256], vs_[0:nt, 0:256])
                            self.cp("dve", Vb[:, tok0 // 128 + tb, half * 256:(half + 1) * 256], vs_[:, 0:256])
                        else:
                            self.dma("sp", self.vs[i, :, c0:c0 + 256], vs_[0:nt, 0:256])
                            self.cp("dve", Vb[0:nt, c0:c0 + 256], vs_[0:nt, 0:256])
                    self.lin_T(cur, self.hT, T, tbs, fv)

    def even_attention_prompt(self, i, T, tok0):
        qT = self.sb("qT", [128, 4, 512], BF16)
        KT = self.sb("KT", [128, 4, self.SEQ], BF16)
        Vb = self.sb("Vb", [128, self.SEQ // 128, 512], BF16)
        kst = [self.sb(f"kst{b}", [128, 512], F32) for b in range(2)]
        vst = [self.sb(f"vst{b}", [128, 256], F32) for b in range(2)]
        e_ = [self.sb(f"e{b}", [128, 512], F32) for b in range(2)]
        sp_ = [self.sb(f"sp{b}", [128, 512], F32) for b in range(2)]
        w_ = [self.sb(f"w{b}", [128, 512], F32) for b in range(2)]
        att_ = [self.sb(f"att{b}", [128, 512], BF16) for b in range(3)]
        S_ = [self.sb(f"Srun{b}", [128, 512], F32) for b in range(2)]
        en = self.sb("en", [128, 512], F32)
        bufs = (e_, sp_, w_, att_, S_, en)
        for hg in range(2):
            if tok0 > 0:
                self.dma("sp", KT[:, :, 0:tok0], self.kts[i, :, hg * 4:(hg + 1) * 4, 0:tok0])
                src = self.vsb.h[i, 0:tok0, hg * 512:(hg + 1) * 512].rearrange("(kb p) c -> p kb c", p=128)
                self.dma("sp", Vb[:, 0:tok0 // 128, :], self.vsb.raw(src, (i * self.SEQ) * 1024, (i * self.SEQ + tok0) * 1024))
            self.even_attn_proj(i, T, 0, tok0, hg, qT, KT, Vb, kst, vst)
            if tok0 + T < self.SEQ:
                self.dma("sp", self.kts[i, :, hg * 4:(hg + 1) * 4, tok0:tok0 + T], KT[:, :, tok0:tok0 + T])
                dst = self.vsb.h[i, tok0:tok0 + T, hg * 512:(hg + 1) * 512].rearrange("(kb p) c -> p kb c", p=128)
                self.dma("sp", self.vsb.raw(dst, (i * self.SEQ + tok0) * 1024, (i * self.SEQ + tok0 + T) * 1024),
                         Vb[:, tok0 // 128:(tok0 + T) // 128, :])
            self.psi = 0
            self.sb_prompt_group(i, hg, T, tok0, KT, Vb, qT, bufs)

    def even_attention_sample(self, i):
        T = TS
        NP = self.NP
        qT = self.sb("qTs", [128, 8, TS], BF16)
        KnT = self.sb("KnT", [128, 8, TS], BF16)
        Vn = self.sb("Vn", [128, 1024], BF16)
        kst = [self.sb(f"kst{b}", [128, 512], F32) for b in range(2)]
        vst = [self.sb(f"vst{b}", [128, 256], F32) for b in range(2)]
        kpg = [self.sb(f"kpg{b}", [128, 1024], BF16) for b in range(4)]
        vpg = [self.sb(f"vpg{b}", [128, 1024], BF16) for b in range(8)]
        z = self.sb("z", [128, 512], F32)
        e = self.sb("e", [128, 512], F32)
        sp = self.sb("sp", [128, 512], F32)
        pf = self.sb("pf", [128, 512], F32)
        w = self.sb("w", [128, 512], F32)
        att = self.sb("att", [128, 512], BF16)
        pre = self.sb("pre", [128, 9, 64], F32)
        bz64 = self.sb("bz64", [128, 64], F32)
        bz = self.sb("bz", [128, 8, 64], F32)
        zer = self.sb("zer", [128, 64], BF16)
        self.memset("pool", zer[:, :], 0.0)
        for hg in range(2):
            self.even_attn_proj(i, T, 1, 0, hg, qT, KnT, Vn, kst, vst)
        tri2, ones, smask = self.C("tri2"), self.C("ones"), self.C("smask")
        sbo = self.poff[f"sb{i}"]
        src = self.par.h[:, sbo:sbo + 8].unsqueeze(2).broadcast_to([128, 8, 8])
        dst = bz64.h.rearrange("p (h t) -> p h t", t=8)
        self.R.op("dve", lambda e_: e_.tensor_copy(out=dst, in_=src), reads=[self.par[:, sbo:sbo + 8]], writes=[bz64[:, :]])
        src2 = bz64.h[:, :].unsqueeze(1).broadcast_to([128, 8, 64])
        self.R.op("dve", lambda e_: e_.tensor_copy(out=bz.h[:, :, :], in_=src2), reads=[bz64[:, :]], writes=[bz[:, :, :]])
        car = pre[:, 8, :]
        pso = self.psum_hold()
        scale = 128 ** -0.5
        first = [True] * 8
        psz = self.psum()
        for h in range(8):
            self.mm(psz[0:8, h * 8:(h + 1) * 8], KnT[:, h, :], qT[:, h, :])
        self.stt(z[0:8, 0:64], psz[0:8, 0:64], scale, bz64[0:8, :], ALU.mult, ALU.add)
        self.act(e[0:8, 0:64], z[0:8, 0:64], AF.Exp)
        self.act(sp[0:8, 0:64], e[0:8, 0:64], AF.Ln, bias=1.0)
        self.tt("pool", e[0:8, 0:64], e[0:8, 0:64], smask[0:8, :], ALU.mult)
        self.tt("pool", sp[0:8, 0:64], sp[0:8, 0:64], smask[0:8, :], ALU.mult)
        ps2 = self.psum()
        self.mm(ps2[:, 0:64], ones[0:8, :], sp[0:8, 0:64])
        ps3 = self.psum()
        self.mm(ps3[0:8, 0:64], tri2[0:8, 0:8], sp[0:8, 0:64])
        self.act(w[0:8, 0:64], ps3[0:8, 0:64], AF.Exp, scale=-1.0)
        self.tt("pool", att[0:8, 0:64], e[0:8, 0:64], w[0:8, 0:64], ALU.mult)
        self.mm(pso[:, 0:64], Vn[0:8, 0:128], zer[0:8, 0:64], start=True, stop=False)
        for h in range(8):
            self.mm(pso[:, h * 8:(h + 1) * 8], Vn[0:8, h * 128:(h + 1) * 128], att[0:8, h * 8:(h + 1) * 8],
                    start=False, stop=False)
        self.cp("dve", car, ps2[:, 0:64])
        ngroups = (NP + 7) // 8
        for gr in reversed(range(ngroups)):
            pages = list(range(gr * 8, min(NP, gr * 8 + 8)))
            npg = len(pages)
            W = npg * 64
            psz = self.psum()
            slots = []
            for k, p in enumerate(pages):
                kb_, vb_ = kpg[p % 4], vpg[p % 8]
                slots.append((kb_, vb_))
                for (buf, cache) in ((kb_, self.cK), (vb_, self.cV)):
                    in_ap = cache.h.rearrange("a r c -> (a r) c")
                    ix = self.idx[:, i, p:p + 1]
                    self.R.op("pool", lambda e_, buf=buf, in_ap=in_ap, ix=ix: e_.indirect_dma_start(
                        out=buf.h[:, :], out_offset=None, in_=in_ap,
                        in_offset=bass.IndirectOffsetOnAxis(ap=ix.ap, axis=0)),
                        reads=[ix], writes=[buf[:, :]], dma=True)
                for h in range(8):
                    self.mm(psz[:, k * 64 + h * 8:k * 64 + h * 8 + 8], kb_[:, h * 128:(h + 1) * 128], qT[:, h, :])
            self.stt(z[:, 0:W], psz[:, 0:W], scale, bz.raw(bz.h.rearrange("p a b -> p (a b)")[:, 0:W], 0, W), ALU.mult, ALU.add)
            self.act(e[:, 0:W], z[:, 0:W], AF.Exp)
            self.act(sp[:, 0:W], e[:, 0:W], AF.Ln, bias=1.0)
            ps2 = self.psum()
            self.mm(ps2[:, 0:W], ones[:, :], sp[:, 0:W])
            ps3 = self.psum()
            self.mm(ps3[:, 0:W], tri2[:, :], sp[:, 0:W])
            self.cp("dve", pre[:, npg - 1, :], car)
            for k in range(npg - 1, 0, -1):
                self.tt("dve", pre[:, k - 1, :], pre[:, k, :], ps2[:, k * 64:(k + 1) * 64], ALU.add)
            self.tt("dve", pf[:, 0:W], ps3[:, 0:W], pre.raw(pre.h.rearrange("p a b -> p (a b)")[:, 0:W], 0, W), ALU.add)
            self.tt("dve", car, pre[:, 0, :], ps2[:, 0:64], ALU.add)
            self.act(w[:, 0:W], pf[:, 0:W], AF.Exp, scale=-1.0)
            self.tt("pool", att[:, 0:W], e[:, 0:W], w[:, 0:W], ALU.mult)
            lastg = (gr == 0)
            for k in range(npg):
                vb_ = slots[k][1]
                for h in range(8):
                    self.mm(pso[:, h * 8:(h + 1) * 8], vb_[:, h * 128:(h + 1) * 128], att[:, k * 64 + h * 8:k * 64 + h * 8 + 8],
                            start=False, stop=(lastg and k == npg - 1))
        o_ap = self.A.h[:, 0:8, 0:8]
        i0 = pso.h[:, 0:64].rearrange("p (h t) -> p h t", t=8)
        self.R.op("dve", lambda e_: e_.tensor_copy(out=o_ap, in_=i0), reads=[pso[:, 0:64]], writes=[self.A[:, 0:8, 0:8]])

    def even_rwkv(self, i, T, g, last):
        Cc = 64 if g == 0 else TS
        CC = 2 * Cc
        NCH = T // Cc
        nsq = int(math.log2(Cc)) - 1
        sb = self.sb
        ST32 = self.ST32[g][i]
        zprev = self.zprev[g][i]
        muo = self.poff[f"mu{i}"]
        blk1 = self.C("blk1")
        ident = self.identb
        rwm = self.C("rwm64" if g == 0 else "rwm8")
        reset = self.C("reset64" if g == 0 else "reset8")
        wa2 = sb("wa2", [128, 1024], BF16)
        g2a = sb("g2a", [128, 1024], BF16)
        g2b = sb("g2b", [32, 1024], BF16)
        self.dma("pool", wa2[:, :], self.lora_d[i, 0:128, :])
        self.dma("pool", g2a[:, :], self.lora_d[i, 128:256, :])
        self.dma("pool", g2b[:, :], self.lora_d[i, 256:288, :])
        tw = sb("tw", [128, 512], BF16)
        sg0 = sb("sg0", [128, 512], BF16)
        sg1 = sb("sg1", [32, 512], BF16)
        zb = [sb(f"zb{b}", [128, 513], F32) for b in range(2)]
        xz = {nm: sb(f"xz_{nm}", [128, 512], F32) for nm in ("r", "k", "v")}
        names = ["ew", "a", "Ln", "eL", "eLm", "eLp", "eLr", "kkn", "kmd", "b", "t1", "t2", "bon", "oT"]
        V = {nm: sb(f"rw_{nm}", [128, 512], F32) for nm in names}
        ARp = sb("ARp", [128, NCH, 2, CC], BF16, aux=True)
        BKp = sb("BKp", [128, NCH, 2, CC], BF16)
        Gp = sb("Gp", [128, NCH, 2, CC], BF16)
        Vp = sb("Vp", [128, NCH, CC], BF16, aux=True)
        S0T = sb("S0T", [128, 128], BF16, aux=True)
        for t_ in (ARp, BKp, Gp, Vp):
            self.memset("pool", t_.full(), 0.0)
        NW = min(4, NCH)
        um = [{nm: sb(f"u{u}_{nm}", [128, 2 * CC if nm in ("P1", "P2", "P3") else 128], BF16, aux=(u < 2))
               for nm in ("P1", "P2", "P3", "N0", "N1", "M0", "M1", "T0", "T1", "ApT", "AT", "AkT", "VpT", "BgT", "KgT", "UT")}
              for u in range(NW)]
        zi = [0]

        def shift(c, ps, npart=128):
            zt = zb[zi[0] % 2]
            zi[0] += 1
            P_ = slice(0, npart)
            self.cp("pool", zt[P_, 0:1], zprev[P_, c:c + 1])
            self.cp("act", zt[P_, 1:T + 1], ps[P_, 0:T])
            self.cp("pool", zprev[P_, c:c + 1], zt[P_, T:T + 1])
            d_ = self.tmp()
            self.tt("dve", d_[P_, 0:T], zt[P_, 0:T], zt[P_, 1:T + 1], ALU.subtract)
            return zt, d_

        def lora_epi(ci, ps):
            if ci == 3:
                return
            c = 24 + ci
            npart = 32 if ci == 2 else 128
            zt, d_ = shift(c, ps, npart)
            P_ = slice(0, npart)
            x_ = self.tmp()
            self.stt(x_[P_, 0:T], d_[P_, 0:T], self.par[P_, muo + c:muo + c + 1], zt[P_, 1:T + 1], ALU.mult, ALU.add)
            if ci == 0:
                self.act(tw[0:64, 0:T], x_[0:64, 0:T], AF.Tanh)
                self.cp("dve", tw[64:128, 0:T], x_[64:128, 0:T])
            elif ci == 1:
                self.act(sg0[:, 0:T], x_[:, 0:T], AF.Sigmoid)
            else:
                self.act(sg1[0:32, 0:T], x_[0:32, 0:T], AF.Sigmoid)

        self.lin_chunks(4, 16, self.hT, T, lora_epi)

        p2 = self.par2
        po = self.poff

        def col(name, hp):
            o = po[f"{name}{i}"] + hp
            return self.par[:, o:o + 1]

        halves = [(slice(0, 64), 0), (slice(64, 128), 1)]

        def pad_write(dst, which, src_a, src_b, negate=False):
            for P_, h2 in halves:
                eng_ = "pool" if h2 == 0 else "dve"
                if which is None:
                    o_ap = dst.h[P_, :, h2 * Cc:(h2 + 1) * Cc]
                else:
                    o_ap = dst.h[P_, :, which, h2 * Cc:(h2 + 1) * Cc]
                a_ap = src_a.h[P_, 0:T].rearrange("p (c i) -> p c i", i=Cc)
                wr = dst.raw(dst.h)
                if src_b is None:
                    self.R.op(eng_, lambda e_, o_ap=o_ap, a_ap=a_ap: e_.tensor_copy(out=o_ap, in_=a_ap),
                              reads=[src_a[:, 0:T]], writes=[wr])
                else:
                    b_ap = src_b.h[P_, 0:T].rearrange("p (c i) -> p c i", i=Cc)
                    self.R.op(eng_, lambda e_, o_ap=o_ap, a_ap=a_ap, b_ap=b_ap: e_.tensor_tensor(
                        out=o_ap, in0=a_ap, in1=b_ap, op=ALU.mult), reads=[src_a[:, 0:T], src_b[:, 0:T]], writes=[wr])

        def pair(hp):
            r, k, v = xz["r"], xz["k"], xz["v"]
            psw = self.psum()
            self.mm(psw[:, 0:T], wa2[0:64, hp * 128:(hp + 1) * 128], tw[0:64, 0:T])
            psa = self.psum()
            self.mm(psa[:, 0:T], wa2[64:128, hp * 128:(hp + 1) * 128], tw[64:128, 0:T])
            e1, ew, a = V["t1"], V["ew"], V["a"]
            self.act(e1[:, 0:T], psw[:, 0:T], AF.Exp, bias=p2[:, i * 16 + hp:i * 16 + hp + 1], scale=-1.0)
            self.act(e1[:, 0:T], e1[:, 0:T], AF.Ln, bias=1.0)
            self.act(ew[:, 0:T], e1[:, 0:T], AF.Exp, bias=-0.5, scale=-1.0)
            self.act(a[:, 0:T], psa[:, 0:T], AF.Sigmoid, bias=col("a0", hp))
            Ln_ = V["Ln"]
            self.R.op("dve", lambda e_: e_.tensor_tensor_scan(out=Ln_.h[:, 0:T], data0=reset.h[:, 0:T], data1=ew.h[:, 0:T],
                                                             initial=0.0, op0=ALU.mult, op1=ALU.add),
                      reads=[reset[:, 0:T], ew[:, 0:T]], writes=[Ln_[:, 0:T]])
            eL, eLm, eLp, eLr, t1, t2 = V["eL"], V["eLm"], V["eLp"], V["eLr"], V["t1"], V["t2"]
            self.act(eL[:, 0:T], Ln_[:, 0:T], AF.Exp, scale=-1.0)
            self.act(eLm[:, 0:T], Ln_[:, 0:T], AF.Exp)
            self.tt("pool", t1[:, 0:T], Ln_[:, 0:T], ew[:, 0:T], ALU.subtract)
            self.act(eLp[:, 0:T], t1[:, 0:T], AF.Exp, scale=-1.0)
            lc = Ln_.h[:, 0:T].rearrange("p (c i) -> p c i", i=Cc)[:, :, Cc - 1:Cc].broadcast_to([128, NCH, Cc])
            o3 = t2.h[:, 0:T].rearrange("p (c i) -> p c i", i=Cc)
            l3 = Ln_.h[:, 0:T].rearrange("p (c i) -> p c i", i=Cc)
            self.R.op("dve", lambda e_: e_.tensor_tensor(out=o3, in0=lc, in1=l3, op=ALU.subtract),
                      reads=[Ln_[:, 0:T]], writes=[t2[:, 0:T]])
            self.act(eLr[:, 0:T], t2[:, 0:T], AF.Exp, scale=-1.0)
            kkn, kmd, b_ = V["kkn"], V["kmd"], V["b"]
            self.ts("dve", kkn[:, 0:T], k[:, 0:T], col("kk", hp))
            self.tt("pool", t1[:, 0:T], kkn[:, 0:T], kkn[:, 0:T], ALU.mult)
            pss = self.psum()
            self.mm(pss[:, 0:T], blk1[:, :], t1[:, 0:T])
            self.ts("dve", t1[:, 0:T], pss[:, 0:T], 1e-24, None, ALU.max)
            self.act(t1[:, 0:T], t1[:, 0:T], AF.Ln)
            self.act(t1[:, 0:T], t1[:, 0:T], AF.Exp, scale=-0.5)
            self.tt("dve", kkn[:, 0:T], kkn[:, 0:T], t1[:, 0:T], ALU.mult)
            self.ts("dve", t2[:, 0:T], a[:, 0:T], col("ka", hp), p2[:, i * 16 + 8 + hp:i * 16 + 9 + hp], ALU.mult, ALU.add)
            self.tt("pool", kmd[:, 0:T], k[:, 0:T], t2[:, 0:T], ALU.mult)
            self.tt("pool", b_[:, 0:T], kkn[:, 0:T], a[:, 0:T], ALU.mult)
            bon = V["bon"]
            self.tt("pool", t1[:, 0:T], r[:, 0:T], kmd[:, 0:T], ALU.mult)
            self.ts("dve", t1[:, 0:T], t1[:, 0:T], col("rk", hp))
            psb_ = self.psum()
            self.mm(psb_[:, 0:T], blk1[:, :], t1[:, 0:T])
            self.tt("dve", bon[:, 0:T], psb_[:, 0:T], v[:, 0:T], ALU.mult)
            self.ts("dve", t2[:, 0:T], kkn[:, 0:T], -1.0)
            pad_write(ARp, 0, t2, eLp)
            pad_write(ARp, 1, r, eL)
            pad_write(BKp, 0, b_, eLm)
            pad_write(BKp, 1, kmd, eLm)
            pad_write(Gp, 0, b_, eLr)
            pad_write(Gp, 1, kmd, eLr)
            pad_write(Vp, None, v, None)
            self.cp("dve", S0T[:, :], ST32[:, hp, :])
            oT = V["oT"]
            m_su_iu = rwm[0:CC, 0:2 * CC]
            m_sl = rwm[0:CC, 256:256 + 2 * CC]
            for w0_ in range(0, NCH, NW):
                cs = list(range(w0_, min(NCH, w0_ + NW)))
                U = {c: um[c - w0_] for c in cs}
                for c in cs:
                    u = U[c]
                    ar = ARp.raw(ARp.h[:, c].rearrange("p a b -> p (a b)"), c * 2 * CC, (c + 1) * 2 * CC)
                    bk = BKp.raw(BKp.h[:, c].rearrange("p a b -> p (a b)"), c * 2 * CC, (c + 1) * 2 * CC)
                    for nm, lh, rh, mk in (("P1", BKp[:, c, 0, :], ar, m_su_iu), ("P2", BKp[:, c, 1, :], ar, m_su_iu),
                                           ("P3", ARp[:, c, 0, :], bk, m_sl)):
                        ps = self.psum()
                        self.mm(ps[0:CC, 0:2 * CC], lh, rh)
                        self.tt("dve", u[nm][0:CC, 0:2 * CC], ps[0:CC, 0:2 * CC], mk, ALU.mult)
                for c in cs:
                    u = U[c]
                    for nm, src in (("ApT", ARp[:, c, 0, :]), ("VpT", Vp[:, c, :]), ("BgT", Gp[:, c, 0, :]), ("KgT", Gp[:, c, 1, :])):
                        pb = self.psumb()
                        self.tr(pb[0:CC, 0:128], src, ident[:, :])
                        self.cp("act", u[nm][0:CC, 0:128], pb[0:CC, 0:128])
                for c in cs:
                    u = U[c]
                    self.tt("pool", u["T0"][0:CC, 0:CC], u["P1"][0:CC, 0:CC], ident[0:CC, 0:CC], ALU.add)
                Ncur = {c: U[c]["P1"][0:CC, 0:CC] for c in cs}
                Mcur = {c: U[c]["P3"][0:CC, 0:CC] for c in cs}
                Tcur = {c: U[c]["T0"] for c in cs}
                for kq in range(nsq):
                    for c in cs:
                        u = U[c]
                        Mn = u["M0"] if kq % 2 == 0 else u["M1"]
                        ps = self.psum()
                        self.mm(ps[0:CC, 0:CC], Ncur[c], Mcur[c])
                        self.cp("act", Mn[0:CC, 0:CC], ps[0:CC, 0:CC])
                        if kq < nsq - 1:
                            Nn = u["N0"] if kq % 2 == 0 else u["N1"]
                            ps2 = self.psum()
                            self.mm(ps2[0:CC, 0:CC], Mcur[c], Ncur[c])
                            self.cp("dve", Nn[0:CC, 0:CC], ps2[0:CC, 0:CC])
                            Ncur[c] = Nn[0:CC, 0:CC]
                        Mcur[c] = Mn[0:CC, 0:CC]
                    for c in cs:
                        u = U[c]
                        Tn_ = u["T1"] if Tcur[c] is u["T0"] else u["T0"]
                        ps = self.psum()
                        self.mm(ps[0:CC, 0:CC], Mcur[c], Tcur[c][0:CC, 0:CC])
                        self.tt("dve", Tn_[0:CC, 0:CC], ps[0:CC, 0:CC], Tcur[c][0:CC, 0:CC], ALU.add)
                        Tcur[c] = Tn_
                for c in cs:
                    u = U[c]
                    ps = self.psum()
                    self.mm(ps[:, 0:CC], u["ApT"][0:CC, 0:128], Tcur[c][0:CC, 0:CC])
                    self.cp("act", u["AT"][:, 0:CC], ps[:, 0:CC])
                    ps2 = self.psum()
                    self.mm(ps2[0:CC, 0:CC], u["P3"][0:CC, CC:2 * CC], Tcur[c][0:CC, 0:CC])
                    self.cp("dve", u["AkT"][0:CC, 0:CC], ps2[0:CC, 0:CC])
                for c in cs:
                    u = U[c]
                    ps = self.psum()
                    self.mm(ps[0:CC, 0:128], u["AT"][:, 0:CC], S0T[:, :], start=True, stop=False)
                    self.mm(ps[0:CC, 0:128], u["AkT"][0:CC, 0:CC], u["VpT"][0:CC, 0:128], start=False, stop=True)
                    self.cp("act", u["UT"][0:CC, 0:128], ps[0:CC, 0:128])
                    pso = self.psum()
                    self.mm(pso[:, 0:CC], S0T[:, :], ARp[:, c, 1, :], start=True, stop=False)
                    self.mm(pso[:, 0:CC], u["UT"][0:CC, 0:128], u["P1"][0:CC, CC:2 * CC], start=False, stop=False)
                    self.mm(pso[:, 0:CC], u["VpT"][0:CC, 0:128], u["P2"][0:CC, CC:2 * CC], start=False, stop=True)
                    self.cp("act", oT[0:64, c * Cc:(c + 1) * Cc], pso[0:64, 0:Cc])
                    self.cp("act", oT[64:128, c * Cc:(c + 1) * Cc], pso[64:128, Cc:2 * Cc])
                    pss_ = self.psum()
                    self.mm(pss_[:, 0:128], u["BgT"][0:CC, 0:128], u["UT"][0:CC, 0:128], start=True, stop=False)
                    self.mm(pss_[:, 0:128], u["KgT"][0:CC, 0:128], u["VpT"][0:CC, 0:128], start=False, stop=True)
                    gcol = eL[:, c * Cc + Cc - 1:c * Cc + Cc]
                    self.stt(ST32[:, hp, :], ST32[:, hp, :], gcol, pss_[:, 0:128], ALU.mult, ALU.add)
                    self.cp("dve", S0T[:, :], ST32[:, hp, :])
            psm = self.psum()
            self.mm(psm[:, 0:T], blk1[:, :], oT[:, 0:T])
            self.tt("pool", t1[:, 0:T], oT[:, 0:T], oT[:, 0:T], ALU.mult)
            psq = self.psum()
            self.mm(psq[:, 0:T], blk1[:, :], t1[:, 0:T])
            self.ts("dve", t2[:, 0:T], psm[:, 0:T], 1.0 / 64)
            self.tt("pool", t1[:, 0:T], t2[:, 0:T], t2[:, 0:T], ALU.mult)
            self.stt(t1[:, 0:T], psq[:, 0:T], 1.0 / 64, t1[:, 0:T], ALU.mult, ALU.subtract)
            self.act(t1[:, 0:T], t1[:, 0:T], AF.Ln, bias=GN_EPS)
            self.act(t1[:, 0:T], t1[:, 0:T], AF.Exp, scale=-0.5)
            self.tt("pool", oT[:, 0:T], oT[:, 0:T], t2[:, 0:T], ALU.subtract)
            self.tt("dve", oT[:, 0:T], oT[:, 0:T], t1[:, 0:T], ALU.mult)
            self.act(oT[:, 0:T], oT[:, 0:T], AF.Identity, bias=col("lb", hp), scale=col("lw", hp))
            self.tt("pool", oT[:, 0:T], oT[:, 0:T], bon[:, 0:T], ALU.add)
            psg = self.psum()
            self.mm(psg[:, 0:T], g2a[:, hp * 128:(hp + 1) * 128], sg0[:, 0:T], start=True, stop=False)
            self.mm(psg[:, 0:T], g2b[0:32, hp * 128:(hp + 1) * 128], sg1[0:32, 0:T], start=False, stop=True)
            self.tt("dve", self.A[:, 8 + hp, 0:T], oT[:, 0:T], psg[:, 0:T], ALU.mult)

        def rkv_epi(ci, ps):
            hp, part = ci // 3, ci % 3
            c = part * 8 + hp
            zt, d_ = shift(c, ps)
            nm = ("r", "k", "v")[part]
            self.stt(xz[nm][:, 0:T], d_[:, 0:T], self.par[:, muo + c:muo + c + 1], zt[:, 1:T + 1], ALU.mult, ALU.add)
            if part == 2:
                pair(hp)

        self.lin_chunks(24, 16, self.hT, T, rkv_epi)
        if last:
            self.dma("sp", (self.shp if g == 0 else self.shs)[i], zprev[:, :])
            self.dma("sp", (self.wkvp if g == 0 else self.wkvs)[i], ST32[:, :, :])

    def even_layer(self, l, T, g, tok0, last):
        i = l // 2
        self.phase()
        self.rmsnorm(T, f"nm{l}", self.hT)
        if g == 0:
            self.even_attention_prompt(i, T, tok0)
        else:
            self.even_attention_sample(i)
        self.phase()
        self.even_rwkv(i, T, g, last)

        def epi_o(ob, ps):
            self.tt("dve", self.xT[:, ob, 0:T], self.xT[:, ob, 0:T], ps[:, 0:T], ALU.add)

        self.lin_chunks(16, 16, self.A, T, epi_o)

    def build(self):
        self.dma("sp", self.par[:, :], self.par_d[:, :])
        self.dma("sp", self.cst[:, :], self.cst_d[:, :])
        self.dma("pool", self.cstb[:, :], self.cstb_d[:, :])
        for i in range(2):
            self.ts("dve", self.par2[:, i * 16:i * 16 + 8], self.P(f"w0{i}", 0, 8), -1.0)
            self.ts("dve", self.par2[:, i * 16 + 8:i * 16 + 16], self.P(f"ka{i}", 0, 8), -1.0, 1.0, ALU.mult, ALU.add)
        for i in range(2):
            self.memset("pool", self.ST32[0][i].full(), 0.0)
            self.memset("pool", self.zprev[0][i][:, :], 0.0)
        for l in range(4):
            self.memset("pool", self.halo[0][l].full(), 0.0)
        self.dma("sp", self.idx[:, 0, :], self.ptab[:, :])
        self.cp("dve", self.idxf[:, :], self.idx[:, 0, :])
        io_, _ = self.coff["iota"]
        self.ts("dve", self.idxf[:, :], self.idxf[:, :], 128.0, self.cst[:, io_:io_ + 1], ALU.mult, ALU.add)
        self.cp("dve", self.idx[:, 0, :], self.idxf[:, :])
        self.ts("dve", self.idxf[:, :], self.idxf[:, :], float(self.NPOOL * 128), None, ALU.add)
        self.cp("dve", self.idx[:, 1, :], self.idxf[:, :])

        tiles = [(0, t) for t in range(self.NT)] + [(1, 0)]
        for g, t in tiles:
            T = 512 if g == 0 else TS
            tok0 = t * 512
            last = (g == 1) or (t == self.NT - 1)
            if g == 0:
                self.dma("sp", self.xT[:, :, :], self.xp[:, :, tok0:tok0 + 512])
            else:
                self.dma("sp", self.xT[:, :, 0:TS], self.xs[:, :, :])
                for i in range(2):
                    s32 = self.ST32[1][i]
                    self.memset("pool", s32.full(), 0.0)
                    self.dma("sp", s32[0:64, :, 0:64], self.swkv[i, 0:64, :, :])
                    self.dma("sp", s32[64:128, :, 64:128], self.swkv[i, 64:128, :, :])
                    self.dma("sp", self.zprev[1][i][:, :], self.sshift[i])
                for l in range(4):
                    self.dma("sp", self.halo[1][l][:, :, :], self.sconv[l])
            for l in range(self.depth):
                if l % 2 == 0:
                    self.even_layer(l, T, g, tok0, last)
                else:
                    self.gmlp(l, T, g)
                self.ffn(l, T, g)
                if last:
                    self.dma("sp", (self.cvp if g == 0 else self.cvs)[l], self.halo[g][l][:, :, :])
            self.phase()
            yT = self.sb("yT", [128, 16, 512], F32)
            self.rmsnorm(T, "nfin", yT)
            if g == 0:
                self.dma("sp", self.yp[:, :, tok0:tok0 + 512], yT[:, :, :])
            else:
                self.dma("sp", self.ys[:, :, :], yT[:, :, 0:TS])
        self.R.emit()
        return self.nc


def prepare_inputs(inp, SEQ, NPAGES, NPOOL, n_cores=8):
    f32 = np.float32
    wst = build_wstream(inp)
    assert wst.shape[0] == NBLK, (wst.shape, NBLK)
    par = build_par(inp)
    (cst, _), (cstb, _) = pack_consts()
    lora = np.zeros((2, 288, 1024), f32)
    for i in range(2):
        lora[i, 0:64] = np.asarray(inp["w2"][i])
        lora[i, 64:128] = np.asarray(inp["a2"][i])
        lora[i, 128:288] = np.asarray(inp["g2"][i])
    wsp = np.ascontiguousarray(np.asarray(inp["w_spatial"], dtype=f32).transpose(0, 1, 3, 2))
    bsp = np.ascontiguousarray(np.broadcast_to(np.asarray(inp["b_spatial"], dtype=f32).reshape(2, 1, 1024), (2, 128, 1024)))
    ck = np.asarray(inp["cache_k"], dtype=f32)
    cv = np.asarray(inp["cache_v"], dtype=f32)
    cK = np.ascontiguousarray(ck.transpose(0, 1, 4, 3, 2)).reshape(2, NPOOL * 128, 1024)
    cV = np.ascontiguousarray(cv).reshape(2, NPOOL * 128, 1024)
    xp = np.asarray(inp["x_prompt"], dtype=f32)
    xs = np.asarray(inp["x_sample"], dtype=f32)
    pt = np.asarray(inp["page_table"]).astype(np.int32)
    ssh = np.asarray(inp["state_shift"], dtype=f32)
    swk = np.asarray(inp["state_wkv"], dtype=f32)
    scv = np.asarray(inp["state_conv"], dtype=f32)
    maps = []
    for c in range(n_cores):
        b = c % xp.shape[0]
        m = {"wst": wst, "par": par, "cst": cst, "cstb": cstb, "lora": lora, "wsp": wsp, "bsp": bsp, "cacheK": cK, "cacheV": cV}
        m["xp"] = np.ascontiguousarray(xp[b].T.reshape(16, 128, SEQ).transpose(1, 0, 2))
        m["xs"] = np.ascontiguousarray(xs[c].T.reshape(16, 128, TS).transpose(1, 0, 2))
        m["ptab"] = np.ascontiguousarray(np.broadcast_to(pt[c][None, :], (128, NPAGES)))
        m["sshift"] = np.stack([vecT(ssh[i, c], 27) for i in range(2)])
        S = swk[:, c].reshape(2, 8, 2, 64, 64)
        m["swkv"] = np.ascontiguousarray(S.transpose(0, 2, 4, 1, 3).reshape(2, 128, 8, 64))
        m["sconv"] = np.ascontiguousarray(scv[:, c].reshape(4, 2, 64, 128).transpose(0, 3, 1, 2))
        maps.append(m)
    return maps


def assemble(res, SEQ, n_prompt=4, n_cores=8):
    f32 = np.float32

    def fm(a):
        return a.transpose(2, 1, 0).reshape(a.shape[2], -1)

    R = res
    y_p = np.stack([fm(R[b]["yp"]) for b in range(n_prompt)])
    y_s = np.stack([fm(R[c]["ys"]) for c in range(n_cores)])
    k_p = np.stack([R[b]["kp"].transpose(0, 3, 2, 1) for b in range(n_prompt)], axis=1)
    v_p = np.stack([R[b]["vp"].reshape(2, SEQ, 8, 128) for b in range(n_prompt)], axis=1)
    k_s = np.stack([R[c]["ks"].transpose(0, 3, 2, 1) for c in range(n_cores)], axis=1)
    v_s = np.stack([R[c]["vs"].reshape(2, TS, 8, 128) for c in range(n_cores)], axis=1)

    def sh(a):
        return a.transpose(0, 2, 1).reshape(2, -1)[:, :RW_IN]

    sh_p = np.stack([sh(R[b]["shp"]) for b in range(n_prompt)], axis=1)
    sh_s = np.stack([sh(R[c]["shs"]) for c in range(n_cores)], axis=1)

    def wkv(a):
        out = np.zeros((2, 16, 64, 64), f32)
        for h2 in range(2):
            blk = a[:, h2 * 64:(h2 + 1) * 64, :, h2 * 64:(h2 + 1) * 64]
            out[:, h2::2] = blk.transpose(0, 2, 3, 1)
        return out

    wkv_p = np.stack([wkv(R[b]["wkvp"]) for b in range(n_prompt)], axis=1)
    wkv_s = np.stack([wkv(R[c]["wkvs"]) for c in range(n_cores)], axis=1)

    def cvf(a):
        return a.transpose(0, 2, 3, 1).reshape(4, 2, 8192)

    cv_p = np.stack([cvf(R[b]["cvp"]) for b in range(n_prompt)], axis=1)
    cv_s = np.stack([cvf(R[c]["cvs"]) for c in range(n_cores)], axis=1)
    chv = np.stack([R[c]["chv"].transpose(0, 3, 2, 1).reshape(2, TS, 4096) for c in range(n_cores)], axis=1)
    outs = (y_p, y_s, k_p, v_p, k_s, v_s, sh_p, sh_s, wkv_p, wkv_s, cv_p, cv_s, chv)
    return tuple(np.ascontiguousarray(o, dtype=f32) for o in outs)


def kernel(**inputs):
    xp = np.asarray(inputs["x_prompt"])
    SEQ = xp.shape[1]
    pt = np.asarray(inputs["page_table"])
    NPAGES = pt.shape[1]
    NPOOL = np.asarray(inputs["cache_k"]).shape[1]
    maps = prepare_inputs(inputs, SEQ, NPAGES, NPOOL)
    nc = Builder(SEQ, NPAGES, NPOOL).build()
    res = run_bass_kernel_spmd(nc, maps, core_ids=list(range(8)))
    return assemble(res.results, SEQ, n_prompt=xp.shape[0])
```
